# Optimizing a Trainium2 kernel written in Bass

```python
import math
import jax, jax.numpy as jnp
from jax import lax
import numpy as np

D_MODEL = 1024
BATCH = 4
SEQ = 8192
DEPTH = 2

PLE_DIM = 256
NORM_EPS = 1e-6
S5_WIDTH = 512
S5_GROUP = 16
S5_GROUPS = S5_WIDTH // S5_GROUP
S5_STATE = 64
S5_DT_MIN = 1e-3
S5_DT_MAX = 1e-1
LRU_WIDTH = 1280
LRU_HEADS = 10
LRU_HEAD_DIM = LRU_WIDTH // LRU_HEADS
LRU_C = 8.0
CONV_WIDTH = 4
IN_SPLITS = (
    S5_WIDTH,
    2 * S5_WIDTH,
    2 * S5_WIDTH + LRU_WIDTH,
    2 * S5_WIDTH + 2 * LRU_WIDTH,
    2 * S5_WIDTH + 2 * LRU_WIDTH + D_MODEL,
)
IN_COLS = 2 * S5_WIDTH + 2 * LRU_WIDTH + 2 * D_MODEL

kernel_name = 'hybrid_s5_rglru_gated_parallel'


def rms_norm(x, g):
    xf = x.astype(jnp.float32)
    y = xf * lax.rsqrt(jnp.mean(xf * xf, axis=-1, keepdims=True) + NORM_EPS)
    return (y * g.astype(jnp.float32)).astype(x.dtype)


def _linear_combine(e1, e2):
    a1, b1 = e1
    a2, b2 = e2
    return (a1 * a2, a2 * b1 + b2)


def s5_ssm(u, a_re, a_im, log_dt, b_re, b_im, c_re, c_im, d_skip):
    f32 = jnp.float32
    bsz, seqlen, _ = u.shape
    uf = u.astype(f32)
    ug = uf.reshape(bsz, seqlen, S5_GROUPS, S5_GROUP)
    lam = lax.complex(a_re.astype(f32), a_im.astype(f32))
    dt = jnp.exp(log_dt.astype(f32))[:, None]
    a_bar = jnp.exp(lam * dt)
    zoh = (a_bar - 1.0) / lam
    b = lax.complex(b_re.astype(f32), b_im.astype(f32))
    b_bar = zoh[..., None] * b
    bu = lax.complex(jnp.einsum('blgc,gnc->lbgn', ug, jnp.real(b_bar)),
                     jnp.einsum('blgc,gnc->lbgn', ug, jnp.imag(b_bar)))
    a_seq = jnp.broadcast_to(a_bar[None, None], (seqlen, 1, S5_GROUPS, S5_STATE))
    _, states = lax.associative_scan(_linear_combine, (a_seq, bu), axis=0)
    y = (jnp.einsum('lbgn,gcn->blgc', jnp.real(states), c_re.astype(f32))
         - jnp.einsum('lbgn,gcn->blgc', jnp.imag(states), c_im.astype(f32)))
    y = y.reshape(bsz, seqlen, S5_WIDTH) + d_skip.astype(f32) * uf
    return y.astype(u.dtype)


def causal_depthwise_conv(x, w, b):
    y = lax.conv_general_dilated(
        x, w[:, None, :].astype(x.dtype), window_strides=(1,),
        padding=[(CONV_WIDTH - 1, 0)],
        dimension_numbers=('NWC', 'WIO', 'NWC'),
        feature_group_count=x.shape[-1])
    return y + b


def rg_lru(x, w_a, b_a, w_x, b_x, lam):
    f32 = jnp.float32
    bsz, seqlen, _ = x.shape
    xf = x.astype(f32)
    xh = xf.reshape(bsz, seqlen, LRU_HEADS, LRU_HEAD_DIM)
    r = jax.nn.sigmoid(jnp.einsum('blhi,hij->blhj', xh, w_a.astype(f32)).reshape(bsz, seqlen, LRU_WIDTH)
                       + b_a.astype(f32))
    i = jax.nn.sigmoid(jnp.einsum('blhi,hij->blhj', xh, w_x.astype(f32)).reshape(bsz, seqlen, LRU_WIDTH)
                       + b_x.astype(f32))
    log_a = -LRU_C * r * jax.nn.softplus(-lam.astype(f32))
    a = jnp.exp(log_a)
    mult = jnp.sqrt(-jnp.expm1(2.0 * log_a))
    _, h = lax.associative_scan(_linear_combine, (a, mult * (i * xf)), axis=1)
    return h.astype(x.dtype)


def setup_inputs(seed: int = 0) -> dict:
    key = jax.random.key(seed)
    ks = jax.random.split(key, 32)
    f32 = jnp.float32

    def nrm(k, shape, scale):
        return jax.random.normal(k, shape, f32) * scale

    L, D, G, N, C, HD = DEPTH, D_MODEL, S5_GROUPS, S5_STATE, S5_GROUP, LRU_HEAD_DIM
    x = nrm(ks[0], (BATCH, SEQ, D), 1.0)
    p = nrm(ks[1], (DEPTH, BATCH, SEQ, PLE_DIM), 1.0)
    g_pre = 1.0 + nrm(ks[2], (L, D), 0.05)
    w_in = nrm(ks[3], (L, D, IN_COLS), D ** -0.5)
    s5_a_re = -0.5 + nrm(ks[4], (L, G, N), 0.01)
    s5_a_im = math.pi * jnp.arange(N, dtype=f32)[None, None, :] + nrm(ks[5], (L, G, N), 0.01)
    s5_log_dt = jax.random.uniform(ks[6], (L, G), f32, math.log(S5_DT_MIN), math.log(S5_DT_MAX))
    s5_b_re = nrm(ks[7], (L, G, N, C), C ** -0.5)
    s5_b_im = nrm(ks[8], (L, G, N, C), C ** -0.5)
    s5_c_re = nrm(ks[9], (L, G, C, N), N ** -0.5)
    s5_c_im = nrm(ks[10], (L, G, C, N), N ** -0.5)
    s5_d = nrm(ks[11], (L, S5_WIDTH), 1.0)
    w_glu = nrm(ks[12], (L, S5_WIDTH, 2 * S5_WIDTH), S5_WIDTH ** -0.5)
    w_bs = nrm(ks[13], (L, S5_WIDTH, D), S5_WIDTH ** -0.5)
    conv_w = nrm(ks[14], (L, CONV_WIDTH, LRU_WIDTH), CONV_WIDTH ** -0.5)
    conv_b = nrm(ks[15], (L, LRU_WIDTH), 0.01)
    lru_w_a = nrm(ks[16], (L, LRU_HEADS, HD, HD), HD ** -0.5)
    lru_b_a = nrm(ks[17], (L, LRU_WIDTH), 0.01)
    lru_w_x = nrm(ks[18], (L, LRU_HEADS, HD, HD), HD ** -0.5)
    lru_b_x = nrm(ks[19], (L, LRU_WIDTH), 0.01)
    a_c = jax.random.uniform(ks[20], (L, LRU_WIDTH), f32, 0.9, 0.999)
    sig = a_c ** (1.0 / LRU_C)
    lru_lambda = jnp.log(sig) - jnp.log1p(-sig)
    w_bl = nrm(ks[21], (L, LRU_WIDTH, D), LRU_WIDTH ** -0.5)
    w_out = nrm(ks[22], (L, D, D), D ** -0.5)
    g_post = 1.0 + nrm(ks[23], (L, D), 0.05)
    w_ple = nrm(ks[24], (L, PLE_DIM, D), PLE_DIM ** -0.5)
    w_ple_gate = nrm(ks[25], (L, D, D), D ** -0.5)
    return {
        'x': x, 'p': p, 'g_pre': g_pre, 'w_in': w_in,
        's5_a_re': s5_a_re, 's5_a_im': s5_a_im, 's5_log_dt': s5_log_dt,
        's5_b_re': s5_b_re, 's5_b_im': s5_b_im, 's5_c_re': s5_c_re, 's5_c_im': s5_c_im,
        's5_d': s5_d, 'w_glu': w_glu, 'w_bs': w_bs,
        'conv_w': conv_w, 'conv_b': conv_b,
        'lru_w_a': lru_w_a, 'lru_b_a': lru_b_a, 'lru_w_x': lru_w_x, 'lru_b_x': lru_b_x,
        'lru_lambda': lru_lambda, 'w_bl': w_bl, 'w_out': w_out, 'g_post': g_post,
        'w_ple': w_ple, 'w_ple_gate': w_ple_gate,
    }


def reference(x, p, g_pre, w_in, s5_a_re, s5_a_im, s5_log_dt, s5_b_re, s5_b_im,
              s5_c_re, s5_c_im, s5_d, w_glu, w_bs, conv_w, conv_b,
              lru_w_a, lru_b_a, lru_w_x, lru_b_x, lru_lambda, w_bl, w_out, g_post,
              w_ple, w_ple_gate):
    for i in range(DEPTH):
        h = rms_norm(x, g_pre[i])
        proj = h @ w_in[i]
        s5_x, s5_g, lru_x, lru_g, gate_s, gate_l = jnp.split(proj, IN_SPLITS, axis=-1)

        y_s = s5_ssm(s5_x, s5_a_re[i], s5_a_im[i], s5_log_dt[i], s5_b_re[i], s5_b_im[i],
                     s5_c_re[i], s5_c_im[i], s5_d[i])
        glu_a, glu_b = jnp.split(jax.nn.gelu(y_s) @ w_glu[i], 2, axis=-1)
        y_s = glu_a * jax.nn.sigmoid(glu_b) * jax.nn.silu(s5_g)
        z_s = y_s @ w_bs[i]

        c = causal_depthwise_conv(lru_x, conv_w[i], conv_b[i])
        y_l = rg_lru(c, lru_w_a[i], lru_b_a[i], lru_w_x[i], lru_b_x[i], lru_lambda[i])
        z_l = (y_l * jax.nn.silu(lru_g)) @ w_bl[i]

        merged = jax.nn.sigmoid(gate_s) * z_s + jax.nn.sigmoid(gate_l) * z_l
        x = x + rms_norm(merged @ w_out[i], g_post[i])

        x = x + (p[i] @ w_ple[i]) * jax.nn.sigmoid(x @ w_ple_gate[i])
    return x
```

```python
import numpy as np
import math
from contextlib import ExitStack
import concourse.bass as bass
import concourse.mybir as mybir
from concourse.bass_utils import run_bass_kernel_spmd

F32 = mybir.dt.float32
BF16 = mybir.dt.bfloat16
I32 = mybir.dt.int32
AF = mybir.ActivationFunctionType
ALU = mybir.AluOpType

D = 1024
SEQ_FULL = 8192
BATCH = 4
PLE = 256
NT = 512
TCH = 8
KC = NT // TCH
NCH = 21
CH_ELEMS = 4096
RING = 4
EPS = 1e-6
TWO_PI = 2.0 * math.pi
CW1 = 6.28125
CW2 = TWO_PI - CW1
PI_SAFE = 3.1415925


class Prog:
    def __init__(self, nc, es):
        self.nc = nc
        self.es = es
        self.ops = []
        self.lastw = {}
        self.readers = {}
        self.bar = None
        self.last_eng = {}
        self.last_dma = {}

    def barrier(self, skip_dma_prefix=()):
        idx = len(self.ops)
        deps = set()
        for e_, i_ in self.last_eng.items():
            deps.add((i_, 'raw'))
        for k_, i_ in self.last_dma.items():
            if any(k_.startswith(pf) for pf in skip_dma_prefix):
                continue
            deps.add((i_, 'raw'))
        self.ops.append(dict(eng='dve', fn=lambda e: e.engine_nop(), deps=deps, dma=None, sig=False))
        self.last_eng['dve'] = idx
        self.bar = idx
        return idx

    def op(self, eng, fn, r=(), w=(), dma=None):
        idx = len(self.ops)
        deps = set()
        if self.bar is not None:
            deps.add((self.bar, 'raw'))
        if dma is None:
            self.last_eng[eng] = idx
        else:
            self.last_dma[dma] = idx
        for k in r:
            if k in self.lastw:
                deps.add((self.lastw[k], 'raw'))
        for k in w:
            if k in self.lastw:
                deps.add((self.lastw[k], 'waw'))
            for rd in self.readers.get(k, ()):
                deps.add((rd, 'war'))
        for k in r:
            self.readers.setdefault(k, []).append(idx)
        for k in w:
            self.lastw[k] = idx
            self.readers[k] = []
        self.ops.append(dict(eng=eng, fn=fn, deps=deps, dma=dma, sig=(dma is not None)))
        return idx

    def finalize(self):
        ops = self.ops
        for i, o in enumerate(ops):
            need = []
            for (p, kind) in o['deps']:
                if p == i:
                    continue
                po = ops[p]
                if po['dma'] is None and po['eng'] == o['eng'] and o['dma'] is None:
                    if o['eng'] == 'pe':
                        continue
                    if kind != 'raw':
                        continue
                need.append(p)
            o['need'] = sorted(set(need))
            for p in o['need']:
                ops[p]['sig'] = True
        EPOCH = 1024
        cnt = {}
        semkeys = set()
        for o in ops:
            if not o['sig']:
                continue
            base = ('dma', o['dma']) if o['dma'] is not None else ('eng', o['eng'])
            inc = 16 if o['dma'] is not None else 1
            c0_ = cnt.get(base, 0)
            ep = c0_ // EPOCH
            cnt[base] = c0_ + inc
            o['sembase'] = base
            o['semep'] = ep
            o['semkey'] = (base, ep)
            o['semval'] = (c0_ % EPOCH) + inc
            o['inc'] = inc
            semkeys.add(o['semkey'])
        self.semkeys = sorted(semkeys, key=str)
        return cnt

    def emit(self):
        nc = self.nc
        sems = {}
        for i, k in enumerate(self.semkeys):
            sems[k] = self.es.enter_context(nc.semaphore("s%d" % i))
        ops = self.ops
        block = self.es.enter_context(nc.Block())

        def emit_engine(engname, e):
            waited = {}
            for o in ops:
                if o['eng'] != engname:
                    continue
                for p in o['need']:
                    po = ops[p]
                    b_ = po['sembase']
                    ev = (po['semep'], po['semval'])
                    if waited.get(b_, (-1, 0)) >= ev:
                        continue
                    e.wait_ge(sems[po['semkey']], po['semval'])
                    waited[b_] = ev
                ins = o['fn'](e)
                if o['sig']:
                    ins.then_inc(sems[o['semkey']], o['inc'])

        @block.sync
        def _(e):
            emit_engine('sp', e)

        @block.scalar
        def _(e):
            emit_engine('act', e)

        @block.gpsimd
        def _(e):
            emit_engine('pool', e)

        @block.tensor
        def _(e):
            emit_engine('pe', e)

        @block.vector
        def _(e):
            emit_engine('dve', e)


WNAMES = ['g_pre', 'w_in', 's5_a_re', 's5_a_im', 's5_log_dt', 's5_b_re', 's5_b_im', 's5_c_re', 's5_c_im',
          's5_d', 'w_glu', 'w_bs', 'conv_w', 'conv_b', 'lru_w_a', 'lru_b_a', 'lru_w_x', 'lru_b_x',
          'lru_lambda', 'w_bl', 'w_out', 'g_post', 'w_ple', 'w_ple_gate']
WSHAPES = {
    'g_pre': [2, 1024], 'w_in': [2, 1024, 5632], 's5_a_re': [2, 32, 64], 's5_a_im': [2, 32, 64],
    's5_log_dt': [2, 32], 's5_b_re': [2, 32, 64, 16], 's5_b_im': [2, 32, 64, 16],
    's5_c_re': [2, 32, 16, 64], 's5_c_im': [2, 32, 16, 64], 's5_d': [2, 512],
    'w_glu': [2, 512, 1024], 'w_bs': [2, 512, 1024], 'conv_w': [2, 4, 1280], 'conv_b': [2, 1280],
    'lru_w_a': [2, 10, 128, 128], 'lru_b_a': [2, 1280], 'lru_w_x': [2, 10, 128, 128], 'lru_b_x': [2, 1280],
    'lru_lambda': [2, 1280], 'w_bl': [2, 1280, 1024], 'w_out': [2, 1024, 1024], 'g_post': [2, 1024],
    'w_ple': [2, 256, 1024], 'w_ple_gate': [2, 1024, 1024],
}


def build_program(SEQ=SEQ_FULL, layers=(0, 1), debug=False):
    nc = bass.Bass("TRN2", target_bir_lowering=False)
    es = ExitStack()
    P = Prog(nc, es)
    NB = SEQ // NT
    NL = len(layers)

    x_in = nc.dram_tensor("x", [SEQ, D], F32, kind="ExternalInput").ap()
    p_in = nc.dram_tensor("p", [2, SEQ, PLE], F32, kind="ExternalInput").ap()
    Wd = {n: nc.dram_tensor(n, WSHAPES[n], F32, kind="ExternalInput").ap() for n in WNAMES}
    out_d = nc.dram_tensor("out", [SEQ, D], F32, kind="ExternalOutput").ap()
    xmid = nc.dram_tensor("xmid", [SEQ, D], F32, kind="Internal").ap() if NL > 1 else None
    WS = [nc.dram_tensor("ws%d" % i, [NCH, 128, CH_ELEMS], BF16, kind="Internal").ap() for i in range(NL)]
    dbg_outs = {}

    sb = lambda name, shape, dt: es.enter_context(nc.sbuf_tensor(name, shape, dt))
    ps = lambda name, shape, dt: es.enter_context(nc.psum_tensor(name, shape, dt))

    def dbg(name, src, shape, deps):
        if not debug:
            return
        d = nc.dram_tensor(name, shape, F32, kind="ExternalOutput").ap()
        dbg_outs[name] = d
        P.op('pool', lambda e: e.dma_start(out=d, in_=src), r=deps, w=[('out', name)], dma='dbg_' + name)

    PSF = ps("PSF", [128, 6, 512], F32)
    PSB = ps("PSB", [128, 2, 1024], BF16)
    pfc = [0]
    pbc = [0]

    def nb():
        i = pfc[0] % 6
        pfc[0] += 1
        return i

    def nb2():
        if pfc[0] % 2:
            pfc[0] += 1
        i = pfc[0] % 6
        pfc[0] += 2
        return i

    def nbb():
        i = pbc[0] % 2
        pbc[0] += 1
        return i

    IDb = sb("IDb", [128, 128], BF16)
    IDf = sb("IDf", [128, 128], F32)
    MASK = sb("MASK", [128, 128], F32)
    MHALF = sb("MHALF", [128, 4], F32)
    GPOST = sb("GPOST", [128, D], F32)
    GPRE = sb("GPRE", [128, 8], F32)
    WV = sb("WV", [128, 32, 2, 64], BF16)
    WTO = sb("WTO", [128, 32, 128], BF16)
    WC = sb("WC", [128, 16, 2, 128], BF16)
    COST = sb("COST", [128, 16, KC], F32)
    SINT = sb("SINT", [128, 16, KC], F32)
    RHO = sb("RHO", [128, 16, KC], F32)
    RHO8 = sb("RHO8", [128, 16], F32)
    CWt = sb("CWt", [128, 10, 4], F32)
    LVEC = sb("LVEC", [128, 6, 10], F32)
    WA = sb("WA", [128, 10, 128], BF16)
    WX = sb("WX", [128, 10, 128], BF16)
    SST = sb("SST", [128, 2, 16], F32)
    HST = sb("HST", [128, 10], F32)
    LXH = sb("LXH", [128, 10, 3], F32)

    ALIAS_BYTES = 61696
    ARENA_BYTES = 128384
    ARENA = sb("ARENA", [128, ARENA_BYTES // 4], F32)

    class Carver:
        def __init__(self, base, limit):
            self.base = base
            self.off = base
            self.limit = limit

        def take(self, shape, dt):
            n = 1
            for s_ in shape[1:]:
                n *= s_
            nbytes = n * (4 if dt in (F32, I32) else 2)
            nbytes = (nbytes + 63) // 64 * 64
            assert self.off + nbytes <= self.limit, (self.off, nbytes, self.limit)
            o4 = self.off // 4
            a = ARENA[0:shape[0], o4:o4 + nbytes // 4]
            if dt != F32:
                a = a.bitcast(dt)
            a = a[:, 0:n]
            self.off += nbytes
            if len(shape) == 2:
                return a
            names = " ".join("d%d" % i for i in range(1, len(shape)))
            kw = {"d%d" % i: shape[i] for i in range(1, len(shape) - 1)}
            return a.rearrange("p (%s) -> p %s" % (names, names), **kw)

    c1 = Carver(0, ALIAS_BYTES)
    X1 = c1.take([128, 4, NT], BF16)
    SG = c1.take([128, 4, NT], BF16)
    X2 = c1.take([128, 32, 8, 16], BF16)
    UP = c1.take([128, 32, KC], BF16)
    Wre = c1.take([128, 16, KC], F32)
    Wim = c1.take([128, 16, KC], F32)
    Ta = c1.take([128, 16, KC], F32)
    Zre = c1.take([128, 16, KC], F32)
    Zim = c1.take([128, 16, KC], F32)
    SPV = c1.take([128, 2, 16, KC], BF16)
    GPP = [c1.take([128, 8, 8, 16], BF16) for _ in range(2)]
    GY = c1.take([128, 4, NT], BF16)
    SIGB = [c1.take([128, NT], F32) for _ in range(2)]
    TG = [c1.take([128, NT], F32) for _ in range(2)]
    TINY = c1.take([128, 4, 16], F32)
    c3 = Carver(0, ALIAS_BYTES)
    SIGS = [c3.take([128, NT], F32) for _ in range(2)]
    SIGL = [c3.take([128, NT], F32) for _ in range(2)]
    TM = [c3.take([128, NT], F32) for _ in range(2)]
    TM2 = [c3.take([128, NT], F32) for _ in range(2)]
    MB = c3.take([128, 8, NT], BF16)
    T1 = [c3.take([128, D], F32) for _ in range(2)]
    PB = c3.take([128, 4, PLE], F32)
    PBb = c3.take([128, PLE], BF16)
    PT = c3.take([128, 2, NT], BF16)
    SGP = [c3.take([128, D], F32) for _ in range(2)]
    cB = Carver(ALIAS_BYTES, ARENA_BYTES)
    XB = cB.take([128, 4, D], F32)
    Hb = [cB.take([128, D], BF16) for i in range(2)]
    JUNK = cB.take([128, D], BF16)
    HT = cB.take([128, 8, NT], BF16)
    X1T = HT
    Y2 = cB.take([128, 4, NT], BF16)
    LX = [cB.take([128, NT + 3], F32) for i in range(2)]
    Cc = [cB.take([128, NT], F32)] * 2
    CBF = [cB.take([128, NT], BF16)] * 2
    Rr = [cB.take([128, NT], F32)] * 2
    Ii = [cB.take([128, NT], F32)] * 2
    Aa = [cB.take([128, NT], F32)] * 2
    A2 = [cB.take([128, NT], F32)] * 2
    Bb = [cB.take([128, NT], F32)] * 2
    HS = [cB.take([128, NT], F32)] * 2
    LGS = [cB.take([128, NT], F32)] * 2
    YL = cB.take([128, 10, NT], BF16)
    print("arena use: c1 %d c3 %d cB %d" % (c1.off, c3.off, cB.off))
    c0 = Carver(0, ARENA_BYTES)
    SS = sb("SS", [128, 8], F32)
    MS = sb("MS", [128, 8], F32)
    RSTD = sb("RSTD", [128, 8], F32)
    WR = sb("WR", [128, RING, CH_ELEMS], BF16)

    TiA = sb("TiA", [128, 128], I32)
    P.op('pool', lambda e: e.iota(TiA[:], pattern=[[1, 128]], base=0, channel_multiplier=-1), w=['TiA'])
    P.op('dve', lambda e: e.tensor_single_scalar(IDf[:], TiA[:], 0.0, ALU.is_equal), r=['TiA'], w=['IDf'])
    P.op('dve', lambda e: e.tensor_copy(IDb[:], IDf[:]), r=['IDf'], w=['IDb'])
    P.op('pool', lambda e: e.iota(TiA[:].rearrange("p (a b) -> p a b", a=8), pattern=[[16, 8], [0, 16]], base=15,
                                  channel_multiplier=-1), r=['IDf'], w=['TiA'])
    P.op('dve', lambda e: e.tensor_single_scalar(MASK[:], TiA[:], 0.0, ALU.is_ge), r=['TiA'], w=['MASK'])
    P.op('dve', lambda e: e.memset(MHALF[:], -0.5), w=['MHALF'])

    def emit_prepass(li, l):
        ws = WS[li]

        def cast(cid, e0, kcn, ncol, src):
            dst = ws[cid, :, e0:e0 + kcn * ncol].rearrange("p (k n) -> p k n", k=kcn)
            s = src.rearrange("(k p) n -> p k n", p=128)
            P.op('pool', lambda e: e.dma_start(out=dst, in_=s), w=[('wsall', li)], dma='pre%d' % li)

        win = Wd['w_in'][l]
        cast(0, 0, 8, 512, win[:, 0:512])
        cast(1, 0, 8, 512, win[:, 512:1024])
        cast(2, 0, 4, 1024, Wd['w_glu'][l])
        for q in range(5):
            cid = 3 + q
            cast(cid, 0, 8, 128, win[:, 1024 + 256 * q:1024 + 256 * q + 128])
            cast(cid, 1024, 8, 128, win[:, 1024 + 256 * q + 128:1024 + 256 * q + 256])
            cast(cid, 2048, 8, 128, win[:, 2304 + 256 * q:2304 + 256 * q + 128])
            cast(cid, 3072, 8, 128, win[:, 2304 + 256 * q + 128:2304 + 256 * q + 256])
        for ot in range(8):
            cid = 8 + ot
            cs = slice(ot * 128, (ot + 1) * 128)
            cast(cid, 0, 4, 128, Wd['w_bs'][l][:, cs])
            cast(cid, 4 * 128, 10, 128, Wd['w_bl'][l][:, cs])
            cast(cid, 14 * 128, 8, 128, win[:, 3584 + ot * 128:3584 + (ot + 1) * 128])
            cast(cid, 22 * 128, 8, 128, win[:, 4608 + ot * 128:4608 + (ot + 1) * 128])
        for h in range(2):
            cast(16 + h, 0, 8, 512, Wd['w_out'][l][:, h * 512:(h + 1) * 512])
            cast(18 + h, 0, 8, 512, Wd['w_ple_gate'][l][:, h * 512:(h + 1) * 512])
        cast(20, 0, 2, 1024, Wd['w_ple'][l])

    chunk_seq = []
    CH_N = [4096, 4096, 4096] + [4096] * 5 + [3840] * 8 + [4096] * 4 + [2048]
    for li in range(NL):
        for b in range(NB):
            for cid in range(NCH):
                chunk_seq.append((li, cid, CH_N[cid]))
    loaded = [0]

    def chunk(gi, base=None):
        if base is None:
            base = gi
        lim = min(base + RING - 1, len(chunk_seq) - 1)
        while loaded[0] <= lim:
            k = loaded[0]
            li, cid, ne = chunk_seq[k]
            slot = k % RING
            src = WS[li][cid, :, 0:ne]
            P.op('sp', (lambda slot, src, ne: lambda e: e.dma_start(out=WR[:, slot, 0:ne], in_=src))(slot, src, ne),
                 r=[('wsall', li)], w=[('WR', slot)], dma='wr%d' % slot)
            loaded[0] += 1
        return gi % RING

    def emit_prologue(li, l):
        c0.off = 0
        P.barrier(skip_dma_prefix=('pre',))
        AR = 'AREN'
        ARE = c0.take([128, 16], F32)
        AIM = c0.take([128, 16], F32)
        DTB = c0.take([128, 16], F32)
        BRE = c0.take([128, 16, 16], F32)
        BIM = c0.take([128, 16, 16], F32)
        CIN = c0.take([128, 2, 2, 2, 64], F32)
        CTR = c0.take([128, 16, 16], F32)
        CTI = c0.take([128, 16, 16], F32)
        DCOL = c0.take([128, 32], F32)
        MULi = c0.take([128, 72], I32)
        MUL = c0.take([128, 72], F32)
        THETA = c0.take([128, 16], F32)
        ARDT = c0.take([128, 16], F32)
        ANG = c0.take([128, 16, 72], F32)
        ANI = c0.take([128, 16, 72], I32)
        ANF = c0.take([128, 16, 72], F32)
        RED = c0.take([128, 16, 72], F32)
        SINA = c0.take([128, 16, 72], F32)
        COSA = c0.take([128, 16, 72], F32)
        EARG = c0.take([128, 16, 8], F32)
        EXPP = c0.take([128, 16, 8], F32)
        EXPM = c0.take([128, 16, 8], F32)
        APr = c0.take([128, 16, 8], F32)
        APi = c0.take([128, 16, 8], F32)
        AIr = c0.take([128, 16, 8], F32)
        AIi = c0.take([128, 16, 8], F32)
        S1 = c0.take([128, 16], F32)
        S2 = c0.take([128, 16], F32)
        S3 = c0.take([128, 16], F32)
        S4 = c0.take([128, 16], F32)
        ZR = c0.take([128, 16], F32)
        ZI = c0.take([128, 16], F32)
        BBr = c0.take([128, 16, 16], F32)
        BBi = c0.take([128, 16, 16], F32)
        Q1 = c0.take([128, 16, 16], F32)
        WINVr = c0.take([128, 16, 8, 16], F32)
        WINVi = c0.take([128, 16, 8, 16], F32)
        WVTr = c0.take([128, 16, 8, 16], F32)
        WVTi = c0.take([128, 16, 8, 16], F32)
        WCr = c0.take([128, 16, 8, 16], F32)
        WCi = c0.take([128, 16, 8, 16], F32)
        Q4 = c0.take([128, 16, 8, 16], F32)
        TMPW = c0.take([128, 128], F32)
        LSC = c0.take([128, 4, 10], F32)

        def dmaP(out, in_, wkeys, nonc=False):
            if nonc:
                P.op('pool', lambda e: e.dma_start(out=out, in_=in_, allow_slow_non_contiguous=True), w=wkeys, dma='prm%d' % li)
            else:
                P.op('pool', lambda e: e.dma_start(out=out, in_=in_), w=wkeys, dma='prm%d' % li)

        for gp in range(2):
            gs = slice(gp * 16, (gp + 1) * 16)
            prt = slice(gp * 64, (gp + 1) * 64)
            dmaP(ARE[prt, :], Wd['s5_a_re'][l, gs, :].rearrange("g n -> n g"), [AR], True)
            dmaP(AIM[prt, :], Wd['s5_a_im'][l, gs, :].rearrange("g n -> n g"), [AR], True)
            dmaP(DTB[prt, :], Wd['s5_log_dt'][l, gs].partition_broadcast(64), [AR])
            dmaP(BRE[prt], Wd['s5_b_re'][l, gs].rearrange("g n c -> n g c"), [AR])
            dmaP(BIM[prt], Wd['s5_b_im'][l, gs].rearrange("g n c -> n g c"), [AR])
            for ri, nm in enumerate(['s5_c_re', 's5_c_im']):
                for ft2 in range(2):
                    g0 = gp * 16 + ft2 * 8
                    dmaP(CIN[:, ri, ft2, gp, :], Wd[nm][l, g0:g0 + 8].rearrange("g c n -> (g c) n"), [AR])
        for j in range(8):
            dmaP(DCOL[j * 16:(j + 1) * 16, :], Wd['s5_d'][l].rearrange("(g c) -> c g", c=16), [AR], True)
        for k in range(4):
            dmaP(CWt[:, :, k], Wd['conv_w'][l, k].rearrange("(t p) -> p t", p=128), ['CWt'], True)
        for i, nm in enumerate(['conv_b', 'lru_b_a', 'lru_b_x', 'lru_lambda']):
            dmaP(LVEC[:, i, :], Wd[nm][l].rearrange("(t p) -> p t", p=128), ['LVEC'], True)
        dmaP(GPRE[:], Wd['g_pre'][l].rearrange("(f p) -> p f", p=128), ['GPRE'], True)
        dmaP(GPOST[:], Wd['g_post'][l].partition_broadcast(128), ['GPOST'])
        dmaP(WA[:], Wd['lru_w_a'][l].rearrange("h i j -> i h j"), ['WA'])
        dmaP(WX[:], Wd['lru_w_x'][l].rearrange("h i j -> i h j"), ['WX'])
        P.barrier(skip_dma_prefix=('pre',))

        def V(fn, r=(AR,), w=(AR,)):
            P.op('dve', fn, r=list(r), w=list(w))

        def A_(fn, r=(AR,), w=(AR,)):
            P.op('act', fn, r=list(r), w=list(w))

        for ri, CT in enumerate([CTR, CTI]):
            for ft2 in range(2):
                bk = nb()
                P.op('pe', (lambda ri, ft2, bk: lambda e: e.transpose(
                    PSF[:, bk, 0:128], CIN[:, ri, ft2].rearrange("p a n -> p (a n)"), IDf[:]))(ri, ft2, bk),
                    r=[AR, 'IDf'], w=[('pf', bk)])
                A_((lambda CT, ft2, bk: lambda e: e.copy(
                    CT[:, ft2 * 8:(ft2 + 1) * 8, :], PSF[:, bk, 0:128].rearrange("p (g c) -> p g c", g=8)))(CT, ft2, bk),
                    r=[('pf', bk)], w=[AR])
        A_(lambda e: e.activation(out=DTB, in_=DTB, func=AF.Exp))
        V(lambda e: e.tensor_tensor(THETA, AIM, DTB, ALU.mult))
        V(lambda e: e.tensor_tensor(ARDT, ARE, DTB, ALU.mult))
        P.op('pool', lambda e: e.iota(MULi[:, 0:8], pattern=[[1, 8]], base=1, channel_multiplier=0), r=[AR], w=[AR])
        P.op('pool', lambda e: e.iota(MULi[:, 8:72], pattern=[[8, 64]], base=8, channel_multiplier=0), r=[AR], w=[AR])
        V(lambda e: e.tensor_copy(MUL, MULi))
        b3 = [128, 16, 72]
        V(lambda e: e.tensor_tensor(ANG, THETA.unsqueeze(2).broadcast_to(b3), MUL.unsqueeze(1).broadcast_to(b3), ALU.mult))

        def sin_of(dst, shift):
            if shift != 0.0:
                V(lambda e: e.tensor_scalar(RED, ANG, shift, None, ALU.add))
                src = RED
            else:
                src = ANG
            V(lambda e: e.tensor_scalar(ANI, src, 1.0 / TWO_PI, None, ALU.mult))
            V(lambda e: e.tensor_copy(ANF, ANI))
            V(lambda e: e.scalar_tensor_tensor(RED.rearrange("p a b -> p (a b)"), ANF.rearrange("p a b -> p (a b)"), -CW1,
                                               src.rearrange("p a b -> p (a b)"), ALU.mult, ALU.add))
            V(lambda e: e.scalar_tensor_tensor(RED.rearrange("p a b -> p (a b)"), ANF.rearrange("p a b -> p (a b)"), -CW2,
                                               RED.rearrange("p a b -> p (a b)"), ALU.mult, ALU.add))
            V(lambda e: e.tensor_scalar(RED, RED, PI_SAFE, -PI_SAFE, ALU.min, ALU.max))
            A_(lambda e: e.activation(out=dst, in_=RED, func=AF.Sin))

        sin_of(SINA, 0.0)
        sin_of(COSA, math.pi / 2)
        b8 = [128, 16, 8]
        V(lambda e: e.tensor_tensor(EARG, ARDT.unsqueeze(2).broadcast_to(b8), MUL[:, 0:8].unsqueeze(1).broadcast_to(b8), ALU.mult))
        A_(lambda e: e.activation(out=EXPP, in_=EARG, func=AF.Exp))
        A_(lambda e: e.activation(out=EXPM, in_=EARG, func=AF.Exp, scale=-1.0))
        V(lambda e: e.tensor_tensor(APr, EXPP, COSA[:, :, 0:8], ALU.mult))
        V(lambda e: e.tensor_tensor(APi, EXPP, SINA[:, :, 0:8], ALU.mult))
        V(lambda e: e.tensor_tensor(AIr, EXPM, COSA[:, :, 0:8], ALU.mult))
        V(lambda e: e.tensor_tensor(AIi, EXPM, SINA[:, :, 0:8], ALU.mult))
        V(lambda e: e.tensor_scalar(AIi, AIi, -1.0, None, ALU.mult))
        V(lambda e: e.tensor_copy(COST[:], COSA[:, :, 8:72]), w=[AR, 'COST'])
        V(lambda e: e.tensor_copy(SINT[:], SINA[:, :, 8:72]), w=[AR, 'SINT'])
        V(lambda e: e.tensor_copy(RHO8[:], EXPP[:, :, 7]), w=[AR, 'RHO8'])
        V(lambda e: e.tensor_copy(RHO[:], EXPP[:, :, 7:8].broadcast_to([128, 16, KC])), w=[AR, 'RHO'])
        V(lambda e: e.memset(RHO[:, :, 0:1], 0.0), w=[AR, 'RHO'])
        V(lambda e: e.tensor_scalar(S1, APr[:, :, 0], -1.0, None, ALU.add))
        V(lambda e: e.tensor_tensor(S2, ARE, ARE, ALU.mult))
        V(lambda e: e.tensor_tensor(S3, AIM, AIM, ALU.mult))
        V(lambda e: e.tensor_tensor(S2, S2, S3, ALU.add))
        V(lambda e: e.reciprocal(S2, S2))
        V(lambda e: e.tensor_tensor(S3, S1, ARE, ALU.mult))
        V(lambda e: e.tensor_tensor(S4, APi[:, :, 0], AIM, ALU.mult))
        V(lambda e: e.tensor_tensor(S3, S3, S4, ALU.add))
        V(lambda e: e.tensor_tensor(ZR, S3, S2, ALU.mult))
        V(lambda e: e.tensor_tensor(S3, APi[:, :, 0], ARE, ALU.mult))
        V(lambda e: e.tensor_tensor(S4, S1, AIM, ALU.mult))
        V(lambda e: e.tensor_tensor(S3, S3, S4, ALU.subtract))
        V(lambda e: e.tensor_tensor(ZI, S3, S2, ALU.mult))
        b16 = [128, 16, 16]
        zr = ZR.unsqueeze(2).broadcast_to(b16)
        zi = ZI.unsqueeze(2).broadcast_to(b16)
        V(lambda e: e.tensor_tensor(BBr, zr, BRE, ALU.mult))
        V(lambda e: e.tensor_tensor(Q1, zi, BIM, ALU.mult))
        V(lambda e: e.tensor_tensor(BBr, BBr, Q1, ALU.subtract))
        V(lambda e: e.tensor_tensor(BBi, zr, BIM, ALU.mult))
        V(lambda e: e.tensor_tensor(Q1, zi, BRE, ALU.mult))
        V(lambda e: e.tensor_tensor(BBi, BBi, Q1, ALU.add))
        b4 = [128, 16, 8, 16]

        def bj(a):
            return a.unsqueeze(3).broadcast_to(b4)

        def bc(a):
            return a.unsqueeze(2).broadcast_to(b4)

        V(lambda e: e.tensor_tensor(WINVr, bj(AIr), bc(BBr), ALU.mult))
        V(lambda e: e.tensor_tensor(Q4, bj(AIi), bc(BBi), ALU.mult))
        V(lambda e: e.tensor_tensor(WINVr, WINVr, Q4, ALU.subtract))
        V(lambda e: e.tensor_tensor(WINVi, bj(AIr), bc(BBi), ALU.mult))
        V(lambda e: e.tensor_tensor(Q4, bj(AIi), bc(BBr), ALU.mult))
        V(lambda e: e.tensor_tensor(WINVi, WINVi, Q4, ALU.add))
        a8r = APr[:, :, 7:8].unsqueeze(3).broadcast_to(b4)
        a8i = APi[:, :, 7:8].unsqueeze(3).broadcast_to(b4)
        V(lambda e: e.tensor_tensor(WVTr, a8r, WINVr, ALU.mult))
        V(lambda e: e.tensor_tensor(Q4, a8i, WINVi, ALU.mult))
        V(lambda e: e.tensor_tensor(WVTr, WVTr, Q4, ALU.subtract))
        V(lambda e: e.tensor_tensor(WVTi, a8r, WINVi, ALU.mult))
        V(lambda e: e.tensor_tensor(Q4, a8i, WINVr, ALU.mult))
        V(lambda e: e.tensor_tensor(WVTi, WVTi, Q4, ALU.add))
        V(lambda e: e.tensor_tensor(WCr, bc(CTR), bj(APr), ALU.mult))
        V(lambda e: e.tensor_tensor(Q4, bc(CTI), bj(APi), ALU.mult))
        V(lambda e: e.tensor_tensor(WCr, WCr, Q4, ALU.subtract))
        V(lambda e: e.tensor_tensor(WCi, bc(CTR), bj(APi), ALU.mult))
        V(lambda e: e.tensor_tensor(Q4, bc(CTI), bj(APr), ALU.mult))
        V(lambda e: e.tensor_tensor(WCi, WCi, Q4, ALU.add))
        V(lambda e: e.tensor_scalar(WCi, WCi, -1.0, None, ALU.mult))
        V(lambda e: e.tensor_copy(WC[:, :, 0, :], WCr.rearrange("p g j c -> p g (j c)")), w=[AR, 'WC'])
        V(lambda e: e.tensor_copy(WC[:, :, 1, :], WCi.rearrange("p g j c -> p g (j c)")), w=[AR, 'WC'])
        for g in range(32):
            gp, gh = g // 16, g % 16
            prt = slice(gp * 64, (gp + 1) * 64)
            bk = nb()
            for ri, (wi, wc) in enumerate([(WINVr, WCr), (WINVi, WCi)]):
                P.op('pe', (lambda wi, wc, bk, ri, gh, prt: lambda e: e.matmul(
                    PSF[:, bk, 0:128], lhsT=wi[prt, gh].rearrange("p j c -> p (j c)"),
                    rhs=wc[prt, gh].rearrange("p j c -> p (j c)"), start=(ri == 0), stop=(ri == 1)))(wi, wc, bk, ri, gh, prt),
                    r=[AR], w=[('pf', bk)])
            V((lambda bk: lambda e: e.tensor_tensor(TMPW, PSF[:, bk, 0:128], MASK[:], ALU.mult))(bk), r=[('pf', bk), 'MASK', AR], w=[AR])
            V((lambda g: lambda e: e.scalar_tensor_tensor(WTO[:, g, :], IDf[:], DCOL[:, g:g + 1], TMPW, ALU.mult, ALU.add))(g),
              r=[AR, 'IDf'], w=[AR, 'WTO'])
        for g0 in range(0, 32, 4):
            bk = nb()
            for gi in range(4):
                g = g0 + gi
                gp, gh = g // 16, g % 16
                prt = slice(gp * 64, (gp + 1) * 64)
                for ri, wv in enumerate([WVTr, WVTi]):
                    P.op('pe', (lambda wv, bk, gi, ri, gh, prt: lambda e: e.transpose(
                        PSF[:, bk, (gi * 2 + ri) * 64:(gi * 2 + ri + 1) * 64],
                        wv[prt, gh].rearrange("p j c -> p (j c)"), IDf[prt, prt]))(wv, bk, gi, ri, gh, prt),
                        r=[AR, 'IDf'], w=[('pf', bk)])
            A_((lambda g0, bk: lambda e: e.copy(WV[:, g0:g0 + 4].rearrange("p g r n -> p (g r n)"), PSF[:, bk, :]))(g0, bk),
               r=[('pf', bk)], w=['WV'])
        lam = LVEC[:, 3, :]
        V(lambda e: e.tensor_scalar(LSC[:, 3, :], lam, -1.0, None, ALU.mult), r=['LVEC', AR])
        V(lambda e: e.tensor_tensor(LSC[:, 0, :], lam, LSC[:, 3, :], ALU.max), r=['LVEC', AR])
        A_(lambda e: e.activation(out=LSC[:, 1, :], in_=LSC[:, 0, :], func=AF.Exp, scale=-1.0))
        A_(lambda e: e.activation(out=LSC[:, 1, :], in_=LSC[:, 1, :], func=AF.Ln, bias=1.0))
        V(lambda e: e.tensor_scalar(LSC[:, 2, :], lam, -1.0, 0.0, ALU.mult, ALU.max), r=['LVEC', AR])
        V(lambda e: e.tensor_tensor(LSC[:, 2, :], LSC[:, 2, :], LSC[:, 1, :], ALU.add))
        V(lambda e: e.tensor_scalar(LVEC[:, 4, :], LSC[:, 2, :], -8.0, None, ALU.mult), w=[AR, 'LVEC'])
        V(lambda e: e.tensor_scalar(LVEC[:, 5, :], LSC[:, 2, :], -16.0, None, ALU.mult), w=[AR, 'LVEC'])
        V(lambda e: e.memset(SST[:], 0.0), r=[], w=['SST'])
        V(lambda e: e.memset(HST[:], 0.0), r=[], w=['HST'])
        V(lambda e: e.memset(LXH[:], 0.0), r=[], w=['LXH'])
        if debug and li == 0:
            dbg("d_wto", WTO[:], [128, 32, 128], ['WTO'])
            dbg("d_wv", WV[:], [128, 32, 2, 64], ['WV'])
            dbg("d_wc", WC[:], [128, 16, 2, 128], ['WC'])
            dbg("d_cos", COST[:], [128, 16, KC], ['COST'])
            dbg("d_sin", SINT[:], [128, 16, KC], ['SINT'])
            dbg("d_rho", RHO[:], [128, 16, KC], ['RHO'])
            dbg("d_lvec", LVEC[:], [128, 6, 10], ['LVEC'])
        P.barrier(skip_dma_prefix=('pre',))

    ARN = 'AREN'

    def emit_block(li, l, b, gbase, x_src, x_dst):
        tok0 = b * NT
        def MM(out, lhsT, rhs, start, stop, r, w):
            P.op('pe', lambda e: e.matmul(out, lhsT=lhsT, rhs=rhs, start=start, stop=stop), r=r, w=w)

        def TR(out, in_, ident, r, w):
            P.op('pe', lambda e: e.transpose(out, in_, ident), r=r, w=w)

        def ACT(out, in_, func, r, w, **kw):
            P.op('act', lambda e: e.activation(out=out, in_=in_, func=func, **kw), r=r, w=w)

        def DV(fn, r, w):
            P.op('dve', fn, r=r, w=w)

        ph1 = [ARN]
        for tt in range(4):
            rows = slice(tok0 + tt * 128, tok0 + (tt + 1) * 128)
            P.op('pool', (lambda tt, rows: lambda e: e.dma_start(out=XB[:, tt, :], in_=x_src[rows, :]))(tt, rows),
                 r=[('xsrc', b, tt)], w=[('XB', tt)], dma='xb%d' % tt)
        for tt in range(4):
            ACT(JUNK, XB[:, tt, :], AF.Square, r=[('XB', tt)], w=['JUNK', ('SS', tt)], accum_out=SS[:, tt:tt + 1])
        DV(lambda e: e.tensor_scalar(MS[:, 0:4], SS[:, 0:4], 1.0 / D, EPS, ALU.mult, ALU.add), r=[('SS', t) for t in range(4)], w=['MSa'])
        P.op('pool', lambda e: e.tensor_tensor(RSTD[:, 0:4], MS[:, 0:4], MHALF[:], ALU.pow), r=['MSa', 'MHALF'], w=['RSTDa'])
        for tt in range(4):
            hb = Hb[tt % 2]
            DV((lambda tt, hb: lambda e: e.tensor_scalar(hb, XB[:, tt, :], RSTD[:, tt:tt + 1], None, ALU.mult))(tt, hb),
               r=[('XB', tt), 'RSTDa'], w=[('Hb', tt % 2)])
            bb = nbb()
            for ft in range(8):
                TR(PSB[:, bb, ft * 128:(ft + 1) * 128], hb[:, ft * 128:(ft + 1) * 128], IDb[:],
                   r=[('Hb', tt % 2), 'IDb'], w=[('pb', bb)])
            DV((lambda tt, bb: lambda e: e.tensor_tensor(
                HT[:, :, tt * 128:(tt + 1) * 128], PSB[:, bb, :].rearrange("p (f t) -> p f t", f=8),
                GPRE[:].unsqueeze(2).broadcast_to([128, 8, 128]), ALU.mult))(tt, bb),
               r=[('pb', bb), 'GPRE'], w=[('HT', tt)])
        HTall = [('HT', t) for t in range(4)]
        dbl = debug and li == 0 and b == 0
        if dbl:
            dbg("b_ht", HT, [128, 8, NT], HTall)

        gi = gbase
        for which in range(2):
            slot = chunk(gi)
            gi += 1
            for ft in range(4):
                bk = nb()
                for kc in range(8):
                    MM(PSF[:, bk, :], WR[:, slot, kc * 512 + ft * 128: kc * 512 + (ft + 1) * 128], HT[:, kc, :],
                       kc == 0, kc == 7, r=[('WR', slot)] + HTall, w=[('pf', bk)])
                if which == 0:
                    ACT(X1[:, ft, :], PSF[:, bk, :], AF.Copy, r=[('pf', bk)], w=[('X1', ft)])
                else:
                    ACT(SG[:, ft, :], PSF[:, bk, :], AF.Silu, r=[('pf', bk)], w=[('SG', ft)])

        for ft in range(4):
            bb = nbb()
            for j in range(8):
                TR(PSB[0:64, bb, j * 128:(j + 1) * 128], X1[:, ft, j:NT:8], IDb[:], r=[('X1', ft), 'IDb'], w=[('pb', bb)])
            DV((lambda ft, bb: lambda e: e.tensor_copy(
                X2[0:64, ft * 8:(ft + 1) * 8].rearrange("k g j c -> k j g c"),
                PSB[0:64, bb, :].rearrange("k (j g c) -> k j g c", j=8, g=8)))(ft, bb),
               r=[('pb', bb)], w=[('X2', ft)])
        for g0 in range(0, 32, 16):
            bb = nbb()
            for gg in range(16):
                g = g0 + gg
                TR(PSB[:, bb, gg * 64:(gg + 1) * 64], X2[0:64, g].rearrange("k j c -> k (j c)"), IDb[0:64, 0:64],
                   r=[('X2', g // 8), 'IDb'], w=[('pb', bb)])
            P.op('act', (lambda g0, bb: lambda e: e.copy(UP[:, g0:g0 + 16, :].rearrange("p g k -> p (g k)"), PSB[:, bb, :]))(g0, bb),
                 r=[('pb', bb)], w=[('UP', g0 // 16)])
        UPall = [('UP', 0), ('UP', 1)]
        if dbl:
            dbg("b_x1", X1, [128, 4, NT], [('X1', f) for f in range(4)])
            dbg("b_sg", SG, [128, 4, NT], [('SG', f) for f in range(4)])
            dbg("b_up", UP, [128, 32, KC], UPall)
        vr = nb2()
        vi = nb2()
        for g in range(32):
            gp, gh = g // 16, g % 16
            prt = slice(gp * 64, (gp + 1) * 64)
            for ri, vb in enumerate([vr, vi]):
                bk = vb + gh // 8
                co = (gh % 8) * 64
                MM(PSF[prt, bk, co:co + 64], WV[:, g, ri, :], UP[:, g, :], True, True,
                   r=['WV', ('UP', g // 16)], w=[('pf', bk)])
        Vre = PSF[:, vr:vr + 2, :]
        Vim = PSF[:, vi:vi + 2, :]
        f2 = lambda a: a.rearrange("p (a b) k -> p a (b k)", a=2)
        Vr_keys = [('pf', vr), ('pf', vr + 1)]
        Vi_keys = [('pf', vi), ('pf', vi + 1)]
        DV(lambda e: e.tensor_tensor(f2(Wre), Vre, f2(COST[:]), ALU.mult), r=Vr_keys + ['COST'], w=['Wre'])
        DV(lambda e: e.tensor_tensor(f2(Ta), Vim, f2(SINT[:]), ALU.mult), r=Vi_keys + ['SINT'], w=['Ta'])
        DV(lambda e: e.tensor_tensor(Wre, Wre, Ta, ALU.add), r=['Wre', 'Ta'], w=['Wre'])
        DV(lambda e: e.tensor_tensor(f2(Wim), Vim, f2(COST[:]), ALU.mult), r=Vi_keys + ['COST'], w=['Wim'])
        DV(lambda e: e.tensor_tensor(f2(Ta), Vre, f2(SINT[:]), ALU.mult), r=Vr_keys + ['SINT', 'Wre'], w=['Ta'])
        DV(lambda e: e.tensor_tensor(Wim, Wim, Ta, ALU.subtract), r=['Wim', 'Ta'], w=['Wim'])
        DV(lambda e: e.tensor_tensor(TINY[:, 0, :], RHO8[:], SST[:, 0, :], ALU.mult), r=['RHO8', 'SST'], w=['TINY0'])
        DV(lambda e: e.tensor_tensor(TINY[:, 1, :], RHO8[:], SST[:, 1, :], ALU.mult), r=['RHO8', 'SST'], w=['TINY1'])
        DV(lambda e: e.tensor_tensor(Wre[:, :, 0], Wre[:, :, 0], TINY[:, 0, :], ALU.add), r=['Wre', 'TINY0'], w=['Wre'])
        DV(lambda e: e.tensor_tensor(Wim[:, :, 0], Wim[:, :, 0], TINY[:, 1, :], ALU.add), r=['Wim', 'TINY1'], w=['Wim'])
        P.op('act', lambda e: e.copy(SPV[:, 0, :, 0], SST[:, 0, :]), r=['SST'], w=['SPVa'])
        P.op('act', lambda e: e.copy(SPV[:, 1, :, 0], SST[:, 1, :]), r=['SST'], w=['SPVb'])
        fl = lambda a: a.rearrange("p g k -> p (g k)")
        DV(lambda e: e.tensor_tensor_scan(fl(Zre), fl(RHO[:]), fl(Wre), 0.0, ALU.mult, ALU.add), r=['RHO', 'Wre'], w=['Zre'])
        DV(lambda e: e.tensor_tensor_scan(fl(Zim), fl(RHO[:]), fl(Wim), 0.0, ALU.mult, ALU.add), r=['RHO', 'Wim'], w=['Zim'])
        DV(lambda e: e.tensor_tensor(Ta, Zre, COST[:], ALU.mult), r=['Zre', 'COST'], w=['Ta'])
        DV(lambda e: e.tensor_tensor(Wre, Zim, SINT[:], ALU.mult), r=['Zim', 'SINT'], w=['Wre'])
        DV(lambda e: e.tensor_tensor(Ta, Ta, Wre, ALU.subtract), r=['Ta', 'Wre'], w=['Ta'])
        P.op('act', lambda e: e.copy(SPV[:, 0, :, 1:KC], Ta[:, :, 0:KC - 1]), r=['Ta'], w=['SPVa2'])
        DV(lambda e: e.tensor_copy(SST[:, 0, :], Ta[:, :, KC - 1]), r=['Ta', 'SPVa', 'TINY0'], w=['SSTa'])
        DV(lambda e: e.tensor_tensor(Wre, Zre, SINT[:], ALU.mult), r=['Zre', 'SINT'], w=['Wre'])
        DV(lambda e: e.tensor_tensor(Wim, Zim, COST[:], ALU.mult), r=['Zim', 'COST'], w=['Wim'])
        DV(lambda e: e.tensor_tensor(Wre, Wre, Wim, ALU.add), r=['Wre', 'Wim'], w=['Wre'])
        P.op('act', lambda e: e.copy(SPV[:, 1, :, 1:KC], Wre[:, :, 0:KC - 1]), r=['Wre'], w=['SPVb2'])
        DV(lambda e: e.tensor_copy(SST[:, 1, :], Wre[:, :, KC - 1]), r=['Wre', 'SPVb', 'TINY1'], w=['SSTb'])
        DV(lambda e: e.tensor_copy(TINY[:, 2, 0:1], SST[:, 0, 0:1]), r=['SSTa', 'SSTb'], w=['SST', 'TINY2'])
        SPVk = ['SPVa', 'SPVb', 'SPVa2', 'SPVb2']
        for ft in range(4):
            yb = nb2()
            for g8 in range(8):
                g = ft * 8 + g8
                gp, gh = g // 16, g % 16
                prt = slice(gp * 64, (gp + 1) * 64)
                bk = yb + g8 // 4
                co = (g8 % 4) * 128
                o = PSF[0:64, bk, co:co + 128]
                MM(o, UP[:, g, :], WTO[:, g, :], True, False, r=[('UP', g // 16), 'WTO'], w=[('pf', bk)])
                MM(o, SPV[prt, 0, gh, :], WC[prt, gh, 0, :], False, False, r=SPVk + ['WC'], w=[('pf', bk)])
                MM(o, SPV[prt, 1, gh, :], WC[prt, gh, 1, :], False, True, r=SPVk + ['WC'], w=[('pf', bk)])
            gpp = GPP[ft % 2]
            ACT(gpp[0:64].rearrange("k j g c -> k g j c"),
                PSF[0:64, yb:yb + 2, :].rearrange("k a (g j c) -> k (a g) j c", g=4, j=8),
                AF.Gelu_apprx_tanh, r=[('pf', yb), ('pf', yb + 1)], w=[('GPP', ft % 2)])
            bb = nbb()
            for j in range(8):
                TR(PSB[:, bb, j * 64:(j + 1) * 64], gpp[0:64, j].rearrange("k g c -> k (g c)"), IDb[0:64, 0:64],
                   r=[('GPP', ft % 2), 'IDb'], w=[('pb', bb)])
            P.op('act', (lambda ft, bb: lambda e: e.copy(
                GY[:, ft, :].rearrange("p (k j) -> p j k", j=8), PSB[:, bb, 0:512].rearrange("p (j k) -> p j k", j=8)))(ft, bb),
                r=[('pb', bb)], w=[('GY', ft)])
        if dbl:
            dbg("b_spv", SPV, [128, 2, 16, KC], SPVk)
            dbg("b_gy", GY, [128, 4, NT], [('GY', f) for f in range(4)])
        slot = chunk(gi)
        gi += 1
        GYall = [('GY', f) for f in range(4)]
        for ft in range(4):
            ba = nb()
            bbk = nb()
            for (bk, ot) in ((ba, ft), (bbk, 4 + ft)):
                for kc in range(4):
                    MM(PSF[:, bk, :], WR[:, slot, kc * 1024 + ot * 128: kc * 1024 + (ot + 1) * 128], GY[:, kc, :],
                       kc == 0, kc == 3, r=[('WR', slot)] + GYall, w=[('pf', bk)])
            sgb = SIGB[ft % 2]
            tg = TG[ft % 2]
            ACT(sgb, PSF[:, bbk, :], AF.Sigmoid, r=[('pf', bbk)], w=[('SIGB', ft % 2)])
            DV((lambda sgb, tg, ft: lambda e: e.tensor_tensor(tg, sgb, SG[:, ft, :], ALU.mult))(sgb, tg, ft),
               r=[('SIGB', ft % 2), ('SG', ft)], w=[('TG', ft % 2)])
            DV((lambda tg, ft, ba: lambda e: e.tensor_tensor(Y2[:, ft, :], PSF[:, ba, :], tg, ALU.mult))(tg, ft, ba),
               r=[('pf', ba), ('TG', ft % 2)], w=[('Y2', ft)])

        for q in range(5):
            slot = chunk(gi)
            gi += 1
            for tl in range(2):
                t = 2 * q + tl
                lp = t % 2
                pp = 0
                lx, cc, cbf, rr, ii, aa, a2, bbv, hs, lgs = LX[lp], Cc[pp], CBF[pp], Rr[pp], Ii[pp], Aa[pp], A2[pp], Bb[pp], HS[pp], LGS[pp]
                bk = nb()
                for kc in range(8):
                    MM(PSF[:, bk, :], WR[:, slot, tl * 1024 + kc * 128: tl * 1024 + (kc + 1) * 128], HT[:, kc, :],
                       kc == 0, kc == 7, r=[('WR', slot)] + HTall, w=[('pf', bk)])
                P.op('act', (lambda lx, bk: lambda e: e.copy(lx[:, 3:NT + 3], PSF[:, bk, :]))(lx, bk), r=[('pf', bk)], w=[('LX', lp)])
                DV((lambda lx, t: lambda e: e.tensor_copy(lx[:, 0:3], LXH[:, t, :]))(lx, t), r=['LXH'], w=[('LXh', lp)])
                DV((lambda lx, t: lambda e: e.tensor_copy(LXH[:, t, :], lx[:, NT:NT + 3]))(lx, t), r=[('LX', lp), ('LXh', lp)], w=['LXH'])
                lxk = [('LX', lp), ('LXh', lp)]
                DV((lambda lx, cc, t: lambda e: e.tensor_scalar(cc, lx[:, 0:NT], CWt[:, t, 0:1], LVEC[:, 0, t:t + 1], ALU.mult, ALU.add))(lx, cc, t),
                   r=lxk + ['CWt', 'LVEC'], w=[('Cc', pp)])
                for k in range(1, 4):
                    DV((lambda lx, cc, t, k: lambda e: e.scalar_tensor_tensor(cc, lx[:, k:NT + k], CWt[:, t, k:k + 1], cc, ALU.mult, ALU.add))(lx, cc, t, k),
                       r=lxk + ['CWt', ('Cc', pp)], w=[('Cc', pp)])
                P.op('act', (lambda cbf, cc: lambda e: e.copy(cbf, cc))(cbf, cc), r=[('Cc', pp)], w=[('CBF', pp)])
                br = nb()
                bi = nb()
                MM(PSF[:, br, :], WA[:, t, :], cbf, True, True, r=['WA', ('CBF', pp)], w=[('pf', br)])
                MM(PSF[:, bi, :], WX[:, t, :], cbf, True, True, r=['WX', ('CBF', pp)], w=[('pf', bi)])
                ACT(rr, PSF[:, br, :], AF.Sigmoid, r=[('pf', br), 'LVEC'], w=[('Rr', pp)], bias=LVEC[:, 1, t:t + 1])
                ACT(ii, PSF[:, bi, :], AF.Sigmoid, r=[('pf', bi), 'LVEC'], w=[('Ii', pp)], bias=LVEC[:, 2, t:t + 1])
                ACT(aa, rr, AF.Exp, r=[('Rr', pp), 'LVEC'], w=[('Aa', pp)], scale=LVEC[:, 4, t:t + 1])
                ACT(a2, rr, AF.Exp, r=[('Rr', pp), 'LVEC'], w=[('A2', pp)], scale=LVEC[:, 5, t:t + 1])
                ACT(a2, a2, AF.Sqrt, r=[('A2', pp)], w=[('A2', pp)], scale=-1.0, bias=1.0)
                DV((lambda bbv, ii, cc: lambda e: e.tensor_tensor(bbv, ii, cc, ALU.mult))(bbv, ii, cc), r=[('Ii', pp), ('Cc', pp)], w=[('Bb', pp)])
                DV((lambda bbv, a2: lambda e: e.tensor_tensor(bbv, bbv, a2, ALU.mult))(bbv, a2), r=[('Bb', pp), ('A2', pp)], w=[('Bb', pp)])
                DV((lambda hs, aa, bbv, t: lambda e: e.tensor_tensor_scan(hs, aa, bbv, HST[:, t:t + 1], ALU.mult, ALU.add))(hs, aa, bbv, t),
                   r=[('Aa', pp), ('Bb', pp), 'HST'], w=[('HS', pp)])
                DV((lambda hs, t: lambda e: e.tensor_copy(HST[:, t:t + 1], hs[:, NT - 1:NT]))(hs, t), r=[('HS', pp)], w=['HST'])
                bg = nb()
                for kc in range(8):
                    MM(PSF[:, bg, :], WR[:, slot, 2048 + tl * 1024 + kc * 128: 2048 + tl * 1024 + (kc + 1) * 128], HT[:, kc, :],
                       kc == 0, kc == 7, r=[('WR', slot)] + HTall, w=[('pf', bg)])
                ACT(lgs, PSF[:, bg, :], AF.Silu, r=[('pf', bg)], w=[('LGS', pp)])
                DV((lambda hs, lgs, t: lambda e: e.tensor_tensor(YL[:, t, :], hs, lgs, ALU.mult))(hs, lgs, t),
                   r=[('HS', pp), ('LGS', pp)], w=[('YL', t)])

        if dbl:
            dbg("b_y2", Y2, [128, 4, NT], [('Y2', f) for f in range(4)])
            dbg("b_yl", YL, [128, 10, NT], [('YL', t) for t in range(10)])
        ph1_keys = [('X1', f) for f in range(4)] + [('SG', f) for f in range(4)] + [('X2', f) for f in range(4)] + UPall + \
            ['Wre', 'Wim', 'Ta', 'Zre', 'Zim'] + SPVk + [('GPP', 0), ('GPP', 1)] + GYall + \
            [('SIGB', 0), ('SIGB', 1), ('TG', 0), ('TG', 1), 'TINY0', 'TINY1', 'TINY2']
        ph3_keys = [('SIGS', 0), ('SIGS', 1), ('SIGL', 0), ('SIGL', 1), ('TM', 0), ('TM', 1), ('TM2', 0), ('TM2', 1)] + \
            [('MB', o_) for o_ in range(8)] + [('T1', 0), ('T1', 1), ('PB', 0), ('PB', 1), ('PB', 2), ('PB', 3), 'PBb', ('PT', 0), ('PT', 1), ('PT', 2), ('PT', 3),
                                                ('SGP', 0), ('SGP', 1)]
        DV(lambda e: e.memset(SS[:, 4:8], 0.0), r=ph1_keys + ph3_keys, w=ph1_keys + ph3_keys + [('SS2', t) for t in range(4)])

        Y2all = [('Y2', f) for f in range(4)]
        YLall = [('YL', t) for t in range(10)]
        for ot in range(8):
            slot = chunk(gi)
            gi += 1
            bz, bl, bs_, bg_ = nb(), nb(), nb(), nb()
            wk = [('WR', slot)]
            for kc in range(4):
                MM(PSF[:, bz, :], WR[:, slot, kc * 128:(kc + 1) * 128], Y2[:, kc, :], kc == 0, kc == 3, r=wk + Y2all, w=[('pf', bz)])
            for kc in range(10):
                MM(PSF[:, bl, :], WR[:, slot, (4 + kc) * 128:(5 + kc) * 128], YL[:, kc, :], kc == 0, kc == 9, r=wk + YLall, w=[('pf', bl)])
            for kc in range(8):
                MM(PSF[:, bs_, :], WR[:, slot, (14 + kc) * 128:(15 + kc) * 128], HT[:, kc, :], kc == 0, kc == 7, r=wk + HTall, w=[('pf', bs_)])
            for kc in range(8):
                MM(PSF[:, bg_, :], WR[:, slot, (22 + kc) * 128:(23 + kc) * 128], HT[:, kc, :], kc == 0, kc == 7, r=wk + HTall, w=[('pf', bg_)])
            pp = ot % 2
            ACT(SIGS[pp], PSF[:, bs_, :], AF.Sigmoid, r=[('pf', bs_)], w=[('SIGS', pp)])
            ACT(SIGL[pp], PSF[:, bg_, :], AF.Sigmoid, r=[('pf', bg_)], w=[('SIGL', pp)])
            DV((lambda pp, bz: lambda e: e.tensor_tensor(TM[pp], PSF[:, bz, :], SIGS[pp], ALU.mult))(pp, bz), r=[('pf', bz), ('SIGS', pp)], w=[('TM', pp)])
            DV((lambda pp, bl: lambda e: e.tensor_tensor(TM2[pp], PSF[:, bl, :], SIGL[pp], ALU.mult))(pp, bl), r=[('pf', bl), ('SIGL', pp)], w=[('TM2', pp)])
            DV((lambda pp, ot: lambda e: e.tensor_tensor(MB[:, ot, :], TM[pp], TM2[pp], ALU.add))(pp, ot), r=[('TM', pp), ('TM2', pp)], w=[('MB', ot)])
        MBall = [('MB', o_) for o_ in range(8)]

        s_out = [chunk(gi), chunk(gi + 1, gi)]
        gi += 2
        for tt in range(4):
            rows = slice(tok0 + tt * 128, tok0 + (tt + 1) * 128)
            P.op('pool', (lambda tt, rows: lambda e: e.dma_start(out=PB[:, tt, :], in_=p_in[l, rows, :]))(tt, rows),
                 w=[('PB', tt)], dma='pb%d' % tt)
        for tt in range(4):
            tk = slice(tt * 128, (tt + 1) * 128)
            mb = nb2()
            for h in range(2):
                for kc in range(8):
                    MM(PSF[:, mb + h, :], MB[:, kc, tk], WR[:, s_out[h], kc * 512:(kc + 1) * 512], kc == 0, kc == 7,
                       r=[('WR', s_out[h])] + MBall, w=[('pf', mb + h)])
            mixk = [('pf', mb), ('pf', mb + 1)]
            mix = PSF[:, mb:mb + 2, :].rearrange("p a n -> p (a n)")
            ACT(JUNK, mix, AF.Square, r=mixk, w=['JUNK', ('SS2', tt)], accum_out=SS[:, 4 + tt:5 + tt])
            DV((lambda tt: lambda e: e.tensor_scalar(MS[:, 4 + tt:5 + tt], SS[:, 4 + tt:5 + tt], 1.0 / D, EPS, ALU.mult, ALU.add))(tt),
               r=[('SS2', tt)], w=[('MSb', tt)])
            P.op('pool', (lambda tt: lambda e: e.tensor_tensor(RSTD[:, 4 + tt:5 + tt], MS[:, 4 + tt:5 + tt], MHALF[:, 0:1], ALU.pow))(tt),
                 r=[('MSb', tt), 'MHALF'], w=[('RSTDb', tt)])
            t1 = T1[tt % 2]
            DV((lambda tt, t1, mix: lambda e: e.scalar_tensor_tensor(t1, mix, RSTD[:, 4 + tt:5 + tt], GPOST[:], ALU.mult, ALU.mult))(tt, t1, mix),
               r=mixk + [('RSTDb', tt), 'GPOST'], w=[('T1', tt % 2)])
            DV((lambda tt, t1: lambda e: e.tensor_tensor(XB[:, tt, :], XB[:, tt, :], t1, ALU.add))(tt, t1),
               r=[('XB', tt), ('T1', tt % 2)], w=[('XB', tt)])
        if dbl:
            dbg("b_mb", MB, [128, 8, NT], MBall)
            dbg("b_x1r", XB, [128, 4, D], [('XB', t) for t in range(4)])
            dbg("b_t1", T1[1], [128, D], [('T1', 1)])
            dbg("b_gpost", GPOST[:], [128, D], ['GPOST'])
        s_pg = [chunk(gi), chunk(gi + 1, gi), chunk(gi + 2, gi)]
        gi += 3
        for tt in range(4):
            rows = slice(tok0 + tt * 128, tok0 + (tt + 1) * 128)
            tk = slice(tt * 128, (tt + 1) * 128)
            t1 = T1[tt % 2]
            hb = Hb[tt % 2]
            P.op('act', (lambda hb, tt: lambda e: e.copy(hb, XB[:, tt, :]))(hb, tt), r=[('XB', tt)], w=[('Hb', tt % 2)])
            bb = nbb()
            for ft in range(8):
                TR(PSB[:, bb, ft * 128:(ft + 1) * 128], hb[:, ft * 128:(ft + 1) * 128], IDb[:], r=[('Hb', tt % 2), 'IDb'], w=[('pb', bb)])
            P.op('act', (lambda tt, bb: lambda e: e.copy(X1T[:, :, tt * 128:(tt + 1) * 128], PSB[:, bb, :].rearrange("p (f t) -> p f t", f=8)))(tt, bb),
                 r=[('pb', bb)], w=[('HT', tt)])
            DV((lambda tt: lambda e: e.tensor_copy(PBb, PB[:, tt, :]))(tt), r=[('PB', tt)], w=['PBb'])
            bb2 = nbb()
            for kc in range(2):
                TR(PSB[:, bb2, kc * 128:(kc + 1) * 128], PBb[:, kc * 128:(kc + 1) * 128], IDb[:], r=['PBb', 'IDb'], w=[('pb', bb2)])
            P.op('act', (lambda tt, bb2: lambda e: e.copy(PT[:, :, tt * 128:(tt + 1) * 128], PSB[:, bb2, 0:256].rearrange("p (f t) -> p f t", f=2)))(tt, bb2),
                 r=[('pb', bb2)], w=[('PT', tt)])
            gb = nb2()
            pbk = nb2()
            for h in range(2):
                for kc in range(8):
                    MM(PSF[:, gb + h, :], X1T[:, kc, tk], WR[:, s_pg[h], kc * 512:(kc + 1) * 512], kc == 0, kc == 7,
                       r=[('WR', s_pg[h]), ('HT', tt)], w=[('pf', gb + h)])
                for kc in range(2):
                    MM(PSF[:, pbk + h, :], PT[:, kc, tk], WR[:, s_pg[2], kc * 1024 + h * 512: kc * 1024 + (h + 1) * 512], kc == 0, kc == 1,
                       r=[('WR', s_pg[2]), ('PT', tt)], w=[('pf', pbk + h)])
            sgp = SGP[tt % 2]
            ACT(sgp, PSF[:, gb:gb + 2, :].rearrange("p a n -> p (a n)"), AF.Sigmoid, r=[('pf', gb), ('pf', gb + 1)], w=[('SGP', tt % 2)])
            DV((lambda t1, sgp, pbk: lambda e: e.tensor_tensor(t1, PSF[:, pbk:pbk + 2, :].rearrange("p a n -> p (a n)"), sgp, ALU.mult))(t1, sgp, pbk),
               r=[('pf', pbk), ('pf', pbk + 1), ('SGP', tt % 2)], w=[('T1', tt % 2)])
            DV((lambda tt, t1: lambda e: e.tensor_tensor(XB[:, tt, :], XB[:, tt, :], t1, ALU.add))(tt, t1),
               r=[('XB', tt), ('T1', tt % 2)], w=[('XB', tt)])
            P.op('pool', (lambda tt, rows: lambda e: e.dma_start(out=x_dst[rows, :], in_=XB[:, tt, :]))(tt, rows),
                 r=[('XB', tt)], w=[('xdst', li, b, tt)], dma='st%d' % tt)
        assert gi == gbase + NCH, (gi, gbase)
        DV(lambda e: e.memset(SS[:, 0:4], 0.0), r=ph1_keys + ph3_keys + [ARN], w=ph1_keys + ph3_keys + [ARN] + [('SS', t) for t in range(4)])

    for li, l in enumerate(layers):
        emit_prepass(li, l)
    for li, l in enumerate(layers):
        x_src = x_in if li == 0 else xmid
        x_dst = out_d if li == NL - 1 else xmid
        emit_prologue(li, l)
        for b in range(NB):
            if li > 0:
                for tt in range(4):
                    P.lastw[('xsrc', b, tt)] = P.lastw[('xdst', li - 1, b, tt)]
            emit_block(li, l, b, (li * NB + b) * NCH, x_src, x_dst)
    fin = [('xdst', NL - 1, b, tt) for b in range(NB) for tt in range(4)] + [('out', n) for n in dbg_outs]
    P.op('pool', lambda e: e.nop(), r=fin)
    P.finalize()
    P.emit()
    es.close()
    return nc, dbg_outs


_CACHE = {}


def kernel(**inputs):
    x = np.ascontiguousarray(inputs['x'], dtype=np.float32)
    p = np.ascontiguousarray(inputs['p'], dtype=np.float32)
    if 'nc' not in _CACHE:
        _CACHE['nc'] = build_program(SEQ_FULL, (0, 1), False)[0]
    nc = _CACHE['nc']
    wmap = {n: np.ascontiguousarray(inputs[n], dtype=np.float32) for n in WNAMES}
    in_maps = []
    for c in range(8):
        b = c % BATCH
        m = dict(wmap)
        m['x'] = x[b]
        m['p'] = np.ascontiguousarray(p[:, b])
        in_maps.append(m)
    res = run_bass_kernel_spmd(nc, in_maps, core_ids=list(range(8)))
    out = np.stack([res.results[b]['out'] for b in range(BATCH)], axis=0)
    return out.astype(np.float32)
```
